# Optimizing a Trainium2 kernel written in Bass

```python
import jax, jax.numpy as jnp
from jax import lax
import numpy as np

D_MODEL = 4096
BATCH = 2
SEQ = 4096
DEPTH = 2

CTX_LEN = 256
GRID_W = 64
HEAD_DIM = 128
EPS = 1e-6
ROPE_THETA = 10000.0
Q_BLOCK = 128
N_MOD = 9
D_FF = 10240
NA_HEADS = 16
NA_WIN_H = 8
NA_WIN_W = 16
GQA_Q_HEADS = 16
GQA_KV_HEADS = 4
GQA_GROUP = GQA_Q_HEADS // GQA_KV_HEADS
NA_WIDTH = NA_HEADS * HEAD_DIM
GQA_Q_WIDTH = GQA_Q_HEADS * HEAD_DIM
GQA_KV_WIDTH = GQA_KV_HEADS * HEAD_DIM
EV_Q_COLS = NA_WIDTH + GQA_Q_WIDTH
EV_KV_COLS = 2 * NA_WIDTH + 2 * GQA_KV_WIDTH
MLA_HEADS = 32
MLA_Q_RANK = 1024
MLA_KV_RANK = 512
MLA_NOPE = 128
MLA_ROPE = 64
MLA_V = 128

kernel_name = "hybrid_natten_gqa_mla_macaron_dit"


def rms_norm(x, g):
    xf = x.astype(jnp.float32)
    y = xf * lax.rsqrt(jnp.mean(xf * xf, axis=-1, keepdims=True) + EPS)
    return (y * g.astype(jnp.float32)).astype(x.dtype)


def modulate(h, g, shift, scale):
    return rms_norm(h, g) * (1.0 + scale) + shift


def swiglu(h, w1, w2):
    gate, up = jnp.split(h @ w1, 2, axis=-1)
    return (jax.nn.silu(gate) * up) @ w2


def heads(t, n):
    return t.reshape(*t.shape[:-1], n, -1)


def axial_angles(n_tok, rot_dim):
    t = jnp.arange(n_tok)
    row = (t // GRID_W).astype(jnp.float32)
    col = (t % GRID_W).astype(jnp.float32)
    axis_dim = rot_dim // 2
    inv_freq = ROPE_THETA ** (-jnp.arange(0, axis_dim, 2, dtype=jnp.float32) / axis_dim)
    ang = jnp.concatenate([row[:, None] * inv_freq, col[:, None] * inv_freq], axis=-1)
    return jnp.cos(ang), jnp.sin(ang)


def apply_rope(x, cos, sin):
    x1, x2 = jnp.split(x, 2, axis=-1)
    cs = cos[None, :, None, :].astype(x.dtype)
    sn = sin[None, :, None, :].astype(x.dtype)
    return jnp.concatenate([x1 * cs - x2 * sn, x1 * sn + x2 * cs], axis=-1)


def sweep_attention(q, k, v):
    B, S = q.shape[0], q.shape[1]
    scale = q.shape[-1] ** -0.5
    nb = S // Q_BLOCK
    qb = q.reshape(B, nb, Q_BLOCK, *q.shape[2:]).swapaxes(0, 1)

    def one_block(qblk):
        s = jnp.einsum("bqhgd,bkhd->bhgqk", qblk, k).astype(jnp.float32) * scale
        p = jax.nn.softmax(s, axis=-1).astype(v.dtype)
        return jnp.einsum("bhgqk,bkhd->bqhgd", p, v)

    ob = lax.map(one_block, qb)
    return ob.swapaxes(0, 1).reshape(B, S, *ob.shape[3:])


def neighbourhood_attention(q, k, v, k_ctx, v_ctx, rpb):
    B, S, H, Dh = q.shape
    rows = S // GRID_W
    kh = min(NA_WIN_H, rows)
    kw = NA_WIN_W
    scale = Dh ** -0.5
    qg = q.reshape(B, rows, GRID_W, H, Dh)
    kg = k.reshape(B, rows, GRID_W, H, Dh)
    vg = v.reshape(B, rows, GRID_W, H, Dh)
    col = jnp.arange(GRID_W)
    c0 = jnp.clip(col - kw // 2, 0, GRID_W - kw)
    col_mask = (col[None, :] >= c0[:, None]) & (col[None, :] < c0[:, None] + kw)
    col_idx = jnp.clip(col[None, :] - col[:, None] + NA_WIN_W - 1, 0, 2 * NA_WIN_W - 2)
    rpb_cols = rpb[:, :, col_idx]
    band = jnp.arange(kh)

    def one_row(r):
        r0 = jnp.clip(r - kh // 2, 0, rows - kh)
        qr = lax.dynamic_index_in_dim(qg, r, axis=1, keepdims=False)
        kb = lax.dynamic_slice_in_dim(kg, r0, kh, axis=1)
        vb = lax.dynamic_slice_in_dim(vg, r0, kh, axis=1)
        bias = jnp.take(rpb_cols, r0 + band - r + NA_WIN_H - 1, axis=1)
        s_band = jnp.einsum("bqhd,bikhd->bhqik", qr, kb).astype(jnp.float32) * scale
        s_band = s_band + bias.transpose(0, 2, 1, 3)[None].astype(jnp.float32)
        s_band = jnp.where(col_mask[None, None, :, None, :], s_band, -jnp.inf)
        s_ctx = jnp.einsum("bqhd,bchd->bhqc", qr, k_ctx).astype(jnp.float32) * scale
        s = jnp.concatenate([s_band.reshape(B, H, GRID_W, kh * GRID_W), s_ctx], axis=-1)
        p = jax.nn.softmax(s, axis=-1).astype(v.dtype)
        p_band = p[..., :kh * GRID_W].reshape(B, H, GRID_W, kh, GRID_W)
        p_ctx = p[..., kh * GRID_W:]
        return (jnp.einsum("bhqik,bikhd->bqhd", p_band, vb)
                + jnp.einsum("bhqc,bchd->bqhd", p_ctx, v_ctx))

    out = lax.map(one_row, jnp.arange(rows))
    return out.transpose(1, 0, 2, 3, 4).reshape(B, S, H, Dh)


def even_mixer(ul, uc, w_in, w_out, na_q_g, na_k_g, rpb, gq_q_g, gq_k_g, cos, sin, ctx_out):
    B, S, _ = ul.shape

    def split_q(p):
        na_q, gq_q = jnp.split(p, [NA_WIDTH], axis=-1)
        return rms_norm(heads(na_q, NA_HEADS), na_q_g), rms_norm(heads(gq_q, GQA_Q_HEADS), gq_q_g)

    def split_kv(p):
        na_k, na_v, gq_k, gq_v = jnp.split(p, [NA_WIDTH, 2 * NA_WIDTH, 2 * NA_WIDTH + GQA_KV_WIDTH], axis=-1)
        return (rms_norm(heads(na_k, NA_HEADS), na_k_g), heads(na_v, NA_HEADS),
                rms_norm(heads(gq_k, GQA_KV_HEADS), gq_k_g), heads(gq_v, GQA_KV_HEADS))

    pl = ul @ w_in
    na_q, gq_q = split_q(pl[..., :EV_Q_COLS])
    na_k, na_v, gq_k, gq_v = split_kv(pl[..., EV_Q_COLS:])
    pc = uc @ (w_in if ctx_out else w_in[:, EV_Q_COLS:])
    na_kc, na_vc, gq_kc, gq_vc = split_kv(pc[..., -EV_KV_COLS:])
    a = neighbourhood_attention(na_q, na_k, na_v, na_kc, na_vc, rpb)
    q_rot = apply_rope(gq_q, cos, sin).reshape(B, S, GQA_KV_HEADS, GQA_GROUP, HEAD_DIM)
    keys = jnp.concatenate([gq_kc, apply_rope(gq_k, cos, sin)], axis=1)
    vals = jnp.concatenate([gq_vc, gq_v], axis=1)
    b = sweep_attention(q_rot, keys, vals)
    out_l = jnp.concatenate([a.reshape(B, S, NA_WIDTH), b.reshape(B, S, GQA_Q_WIDTH)], axis=-1) @ w_out
    if not ctx_out:
        return out_l, None
    L = uc.shape[1]
    na_qc, gq_qc = split_q(pc[..., :EV_Q_COLS])
    ac = sweep_attention(na_qc[:, :, :, None, :], na_kc, na_vc)
    bc = sweep_attention(gq_qc.reshape(B, L, GQA_KV_HEADS, GQA_GROUP, HEAD_DIM), gq_kc, gq_vc)
    out_c = jnp.concatenate([ac.reshape(B, L, NA_WIDTH), bc.reshape(B, L, GQA_Q_WIDTH)], axis=-1) @ w_out
    return out_l, out_c


def mla_mixer(ul, uc, w_down, q_a_g, kv_a_g, w_uq, w_ukv, qn_g, qr_g, kn_g, kr_g, w_o, cos, sin, ctx_out):
    B, S, _ = ul.shape

    def queries(cq):
        q = heads(rms_norm(cq, q_a_g) @ w_uq, MLA_HEADS)
        q_nope, q_rope = jnp.split(q, [MLA_NOPE], axis=-1)
        return rms_norm(q_nope, qn_g), rms_norm(q_rope, qr_g)

    def keys_values(ckv, k_rope):
        kv = heads(rms_norm(ckv, kv_a_g) @ w_ukv, MLA_HEADS)
        k_nope, v = jnp.split(kv, [MLA_NOPE], axis=-1)
        return rms_norm(k_nope, kn_g), rms_norm(k_rope, kr_g)[:, :, None, :], v

    def join(nope, rope):
        return jnp.concatenate([nope, jnp.broadcast_to(rope, nope.shape[:-1] + rope.shape[-1:])], axis=-1)

    dl = ul @ w_down
    q_nope, q_rope = queries(dl[..., :MLA_Q_RANK])
    k_nope, k_rope, v = keys_values(dl[..., MLA_Q_RANK:MLA_Q_RANK + MLA_KV_RANK], dl[..., MLA_Q_RANK + MLA_KV_RANK:])
    dc = uc @ (w_down if ctx_out else w_down[:, MLA_Q_RANK:])
    kc_nope, kc_rope, vc = keys_values(dc[..., -(MLA_KV_RANK + MLA_ROPE):-MLA_ROPE], dc[..., -MLA_ROPE:])
    kc = join(kc_nope, kc_rope)
    q = jnp.concatenate([q_nope, apply_rope(q_rope, cos, sin)], axis=-1)[:, :, :, None, :]
    k = join(k_nope, apply_rope(k_rope, cos, sin))
    o = sweep_attention(q, jnp.concatenate([kc, k], axis=1), jnp.concatenate([vc, v], axis=1))
    out_l = o.reshape(B, S, MLA_HEADS * MLA_V) @ w_o
    if not ctx_out:
        return out_l, None
    L = uc.shape[1]
    qc_nope, qc_rope = queries(dc[..., :MLA_Q_RANK])
    qc = jnp.concatenate([qc_nope, qc_rope], axis=-1)[:, :, :, None, :]
    oc = sweep_attention(qc, kc, vc)
    out_c = oc.reshape(B, L, MLA_HEADS * MLA_V) @ w_o
    return out_l, out_c


def setup_inputs(seed: int = 0) -> dict:
    key = jax.random.key(seed)
    k = jax.random.split(key, 26)
    n_even = (DEPTH + 1) // 2
    n_odd = DEPTH // 2
    f32 = jnp.float32

    def dense(kk, shape, fan_in, mult=1.0):
        return jax.random.normal(kk, shape, f32) * (mult * fan_in ** -0.5)

    def gain(kk, shape):
        return 1.0 + 0.02 * jax.random.normal(kk, shape, f32)

    return {
        "x": jax.random.normal(k[0], (BATCH, SEQ, D_MODEL), f32),
        "c": jax.random.normal(k[1], (BATCH, D_MODEL), f32),
        "ctx": jax.random.normal(k[2], (BATCH, CTX_LEN, D_MODEL), f32),
        "c_ctx": jax.random.normal(k[3], (D_MODEL,), f32),
        "norm_g": gain(k[4], (DEPTH, 3, D_MODEL)),
        "w_mod": dense(k[5], (DEPTH, D_MODEL, N_MOD * D_MODEL), D_MODEL, 0.5),
        "b_mod": 0.02 * jax.random.normal(k[6], (DEPTH, N_MOD * D_MODEL), f32),
        "ffn_w1": dense(k[7], (DEPTH, 2, D_MODEL, 2 * D_FF), D_MODEL),
        "ffn_w2": dense(k[8], (DEPTH, 2, D_FF, D_MODEL), D_FF),
        "ev_w_in": dense(k[9], (n_even, D_MODEL, EV_Q_COLS + EV_KV_COLS), D_MODEL),
        "ev_w_out": dense(k[10], (n_even, NA_WIDTH + GQA_Q_WIDTH, D_MODEL), NA_WIDTH + GQA_Q_WIDTH),
        "na_q_g": gain(k[11], (n_even, HEAD_DIM)),
        "na_k_g": gain(k[12], (n_even, HEAD_DIM)),
        "na_rpb": 0.1 * jax.random.normal(k[13], (n_even, NA_HEADS, 2 * NA_WIN_H - 1, 2 * NA_WIN_W - 1), f32),
        "gq_q_g": gain(k[14], (n_even, HEAD_DIM)),
        "gq_k_g": gain(k[15], (n_even, HEAD_DIM)),
        "mla_w_down": dense(k[16], (n_odd, D_MODEL, MLA_Q_RANK + MLA_KV_RANK + MLA_ROPE), D_MODEL),
        "mla_q_a_g": gain(k[17], (n_odd, MLA_Q_RANK)),
        "mla_kv_a_g": gain(k[18], (n_odd, MLA_KV_RANK)),
        "mla_w_uq": dense(k[19], (n_odd, MLA_Q_RANK, MLA_HEADS * (MLA_NOPE + MLA_ROPE)), MLA_Q_RANK),
        "mla_w_ukv": dense(k[20], (n_odd, MLA_KV_RANK, MLA_HEADS * (MLA_NOPE + MLA_V)), MLA_KV_RANK),
        "mla_qn_g": gain(k[21], (n_odd, MLA_NOPE)),
        "mla_qr_g": gain(k[22], (n_odd, MLA_ROPE)),
        "mla_kn_g": gain(k[23], (n_odd, MLA_NOPE)),
        "mla_kr_g": gain(k[24], (n_odd, MLA_ROPE)),
        "mla_w_o": dense(k[25], (n_odd, MLA_HEADS * MLA_V, D_MODEL), MLA_HEADS * MLA_V),
    }


def reference(x, c, ctx, c_ctx, norm_g, w_mod, b_mod, ffn_w1, ffn_w2, ev_w_in, ev_w_out, na_q_g, na_k_g,
              na_rpb, gq_q_g, gq_k_g, mla_w_down, mla_q_a_g, mla_kv_a_g, mla_w_uq, mla_w_ukv, mla_qn_g,
              mla_qr_g, mla_kn_g, mla_kr_g, mla_w_o):
    S = x.shape[1]
    cos_h, sin_h = axial_angles(S, HEAD_DIM)
    cos_r, sin_r = axial_angles(S, MLA_ROPE)
    hl, hc = x, ctx
    for i in range(DEPTH):
        last = i == DEPTH - 1
        ml = [m[:, None, :] for m in jnp.split(jax.nn.silu(c) @ w_mod[i] + b_mod[i], N_MOD, axis=-1)]
        mc = jnp.split(jax.nn.silu(c_ctx) @ w_mod[i] + b_mod[i], N_MOD, axis=-1)
        g = norm_g[i]
        hl = hl + 0.5 * ml[2] * swiglu(modulate(hl, g[0], ml[0], ml[1]), ffn_w1[i, 0], ffn_w2[i, 0])
        hc = hc + 0.5 * mc[2] * swiglu(modulate(hc, g[0], mc[0], mc[1]), ffn_w1[i, 0], ffn_w2[i, 0])
        ul = modulate(hl, g[1], ml[3], ml[4])
        uc = modulate(hc, g[1], mc[3], mc[4])
        j = i // 2
        if i % 2 == 0:
            ol, oc = even_mixer(ul, uc, ev_w_in[j], ev_w_out[j], na_q_g[j], na_k_g[j], na_rpb[j],
                                gq_q_g[j], gq_k_g[j], cos_h, sin_h, not last)
        else:
            ol, oc = mla_mixer(ul, uc, mla_w_down[j], mla_q_a_g[j], mla_kv_a_g[j], mla_w_uq[j], mla_w_ukv[j],
                               mla_qn_g[j], mla_qr_g[j], mla_kn_g[j], mla_kr_g[j], mla_w_o[j],
                               cos_r, sin_r, not last)
        hl = hl + ml[5] * ol
        hl = hl + 0.5 * ml[8] * swiglu(modulate(hl, g[2], ml[6], ml[7]), ffn_w1[i, 1], ffn_w2[i, 1])
        if not last:
            hc = hc + mc[5] * oc
            hc = hc + 0.5 * mc[8] * swiglu(modulate(hc, g[2], mc[6], mc[7]), ffn_w1[i, 1], ffn_w2[i, 1])
    return hl
```

```python
from contextlib import ExitStack
import numpy as np
import ml_dtypes
import concourse.bass as bass
import concourse.mybir as mybir
from concourse.bass_utils import run_bass_kernel_spmd

F32 = mybir.dt.float32
BF16 = mybir.dt.bfloat16
AF = mybir.ActivationFunctionType
ALU = mybir.AluOpType
NCORES = 8
EPS = 1e-6


class Cfg:
    def __init__(self, D=4096, F=10240, S=4096, CTX=256, L=2, TW=256):
        self.D, self.F, self.S, self.CTX, self.L = D, F, S, CTX, L
        self.KD, self.FC = D // 128, F // 128
        self.TL, self.TC = S // 4, CTX // 4
        self.T = self.TL + self.TC
        self.RPC = self.TL // 64
        self.MT = 9 * self.KD // 8
        assert 9 * self.KD % 8 == 0 and self.FC % 8 == 0 and self.KD % 8 == 0
        lt = []
        c = 0
        while c < self.TL:
            w = min(TW, self.TL - c)
            lt.append((c, c + w, 0))
            c += w
        self.ltiles = lt
        self.ctile = (self.TL, self.T, 1)
        self.tiles = lt + [self.ctile]


class DSem:
    def __init__(self, sem):
        self.sem, self.n = sem, 0


class Buf:
    __slots__ = ("t", "w", "r", "ds", "tcc")

    def __init__(self, t):
        self.t, self.w, self.r, self.ds, self.tcc = t, {}, {}, None, t


def _merge(d, s):
    for k, (sem, v) in s.items():
        if k not in d or d[k][1] < v:
            d[k] = (sem, v)


class Eng:
    def __init__(self, kb, e, name, same):
        self.kb, self.e, self.same = kb, e, same
        self.sem = kb.nc.semaphore("e_" + name).__enter__()
        self.n = 0
        self.seen = {}

    def wait(self, deps):
        for k, (s, v) in deps.items():
            if s is self.sem:
                if v > self.n or not self.same:
                    continue
            if self.seen.get(k, 0) < v:
                self.e.wait_ge(s, v)
                self.seen[k] = v

    def op(self, fn, reads=(), writes=(), signal=True):
        deps = {}
        for b in reads:
            _merge(deps, b.w)
        for b in writes:
            _merge(deps, b.w)
            _merge(deps, b.r)
        self.wait(deps)
        ins = fn(self.e)
        if signal:
            self.n += 1
            ins.then_inc(self.sem, 1)
            v = self.n
        else:
            v = self.n + 1
        k = id(self.sem)
        tag = (self.sem, v)
        for b in reads:
            if k not in b.r or b.r[k][1] < v:
                b.r[k] = tag
        for b in writes:
            b.w = {k: tag}
            b.r = {}
        return ins

    def dma(self, out_b, out_ap, in_b, in_ap, semb, merge=False):
        deps = {}
        _merge(deps, in_b.w)
        _merge(deps, out_b.r)
        if not merge:
            _merge(deps, out_b.w)
        self.wait(deps)
        ins = self.e.dma_start(out=out_ap, in_=in_ap)
        kind = "sw" if self is self.kb.POOL else "hw"
        if semb.ds is None:
            semb.ds = {}
        if kind not in semb.ds:
            semb.ds[kind] = self.kb.get_dsem(kind)
        ds = semb.ds[kind]
        ds.n += 16
        ins.then_inc(ds.sem, 16)
        k = id(ds.sem)
        tag = (ds.sem, ds.n)
        if merge:
            out_b.w[k] = tag
        else:
            out_b.w = {k: tag}
            out_b.r = {}
        in_b.r[k] = tag


class KB:
    def __init__(self, cfg):
        self.cfg = cfg
        self.nc = nc = bass.Bass("TRN2", target_bir_lowering=False)
        self.PE = Eng(self, nc.tensor, "pe", False)
        self.ACT = Eng(self, nc.scalar, "act", True)
        self.DVE = Eng(self, nc.vector, "dve", True)
        self.POOL = Eng(self, nc.gpsimd, "pool", True)
        self.SP = Eng(self, nc.sync, "sp", False)
        self.engs = [self.PE, self.ACT, self.DVE, self.POOL, self.SP]
        self.free_ds = {"hw": [], "sw": []}
        self.n_ds = 0
        self.ps = [Buf(nc.psum_tensor(f"psb{i}", [128, 512], F32).__enter__()) for i in range(8)]
        self.ncc = 0
        self.uid = 0
        self.ccscr = Buf(nc.sbuf_tensor("ccscr", [128, 8], F32).__enter__())

    def get_dsem(self, kind):
        if self.free_ds[kind]:
            return self.free_ds[kind].pop()
        self.n_ds += 1
        return DSem(self.nc.semaphore(f"d{kind}{self.n_ds}").__enter__())

    def dram(self, name, shape, dt, kind=None):
        if kind is None:
            return Buf(self.nc.dram_tensor(name, list(shape), dt))
        return Buf(self.nc.dram_tensor(name, list(shape), dt, kind=kind))

    def sb(self, st, name, shape, dt):
        self.uid += 1
        b = Buf(st.enter_context(self.nc.sbuf_tensor(f"{name}_{self.uid}", list(shape), dt)))
        st.bufs.append(b)
        return b

    def stage(self):
        st = ExitStack()
        st.bufs = []
        return st

    def end_stage(self, st):
        deps = {}
        for e in self.engs:
            if e.n > 0:
                deps[id(e.sem)] = (e.sem, e.n)
        for b in st.bufs:
            _merge(deps, b.w)
            _merge(deps, b.r)
        for e in self.engs:
            e.wait(deps)
        for b in st.bufs:
            if b.ds is not None:
                for kind, d in b.ds.items():
                    self.free_ds[kind].append(d)
        st.close()

    def allgather(self, in_b, out_b):
        P = self.POOL
        deps = {}
        _merge(deps, in_b.w)
        _merge(deps, out_b.r)
        _merge(deps, out_b.w)
        P.wait(deps)
        self.ncc += 1
        sem = self.nc.semaphore(f"cc{self.ncc}").__enter__()
        ins = P.e.collective_compute("AllGather", ALU.bypass, replica_groups=[list(range(NCORES))],
                                     ins=[in_b.tcc.ap().opt()], outs=[out_b.tcc.ap().opt()])
        ins.then_inc(sem)
        P.e.wait_ge(sem, 1)
        P.op(lambda e: e.memset(self.ccscr.t[:], 0.0), [in_b], [out_b, self.ccscr])


def fam_specs(cfg):
    KD, FC = cfg.KD, cfg.FC
    sp = {}
    for i in range(cfg.L):
        for j in range(2):
            sp[f"w1_{i}{j}"] = (FC, KD * 256)
            sp[f"w2_{i}{j}"] = (KD, FC * 128)
    sp["win_a"] = (56, KD * 128)
    sp["win_v"] = (8, KD * 512)
    sp["wout"] = (KD, 32 * 128)
    sp["wdown"] = (16, KD * 128)
    sp["wuq"] = (48, 8 * 128)
    sp["wukv_k"] = (32, 4 * 128)
    sp["wukv_v"] = (8, 4 * 512)
    sp["wo"] = (KD, 32 * 128)
    return sp


FAM_ORDER = ["w1_00", "w2_00", "win_a", "win_v", "wout", "w1_01", "w2_01", "w1_10", "w2_10",
             "wdown", "wuq", "wukv_k", "wukv_v", "wo", "w1_11", "w2_11"]


def tile_ap(gb, t, X):
    rpt = 128 * X // 512
    return gb.t[t * rpt:(t + 1) * rpt, :].rearrange("(p a) c -> p (a c)", p=128)


def build(cfg, stop_after=None, dbg=('mod','modag','cast','castag','ffn','f2','f3','f3b','f3c')):
    kb = KB(cfg)
    nc = kb.nc
    PE, ACT, DVE, POOL, SP = kb.PE, kb.ACT, kb.DVE, kb.POOL, kb.SP
    D, F, KD, FC, T, TL, TC, L, MT = cfg.D, cfg.F, cfg.KD, cfg.FC, cfg.T, cfg.TL, cfg.TC, cfg.L, cfg.MT
    specs = fam_specs(cfg)

    xT = kb.dram("xT", [D, T], F32, "ExternalInput")
    cv = kb.dram("cv", [128, KD * 3], F32, "ExternalInput")
    sel = kb.dram("sel", [128, 20], F32, "ExternalInput")
    wmod = kb.dram("wmod", [L * MT * 128, KD * 128], F32, "ExternalInput")
    bmod = kb.dram("bmod", [128, L * MT], F32, "ExternalInput")
    ng = kb.dram("ng", [128, L * 3 * KD], F32, "ExternalInput")
    consts = kb.dram("consts", [128, 4 * 128], F32, "ExternalInput")
    fin = {}
    bounce = {}
    gath = {}
    for f in FAM_ORDER:
        nt, X = specs[f]
        rows = (nt // 8) * 128 * X // 512
        fin[f] = kb.dram("f_" + f, [rows, 512], F32, "ExternalInput")
        bounce[f] = kb.dram("b_" + f, [rows, 256], F32)
        gath[f] = kb.dram("g_" + f, [8 * rows, 256], F32)
        for bb in (bounce[f], gath[f]):
            bb.tcc = bb.t
            bb.t = bb.t.bitcast(BF16)
    outT = kb.dram("outT", [D, TL], F32, "ExternalOutput")
    dbgo = kb.dram("dbgo", [128, 2048], F32, "ExternalOutput") if "dump" in dbg else None

    hs = [xT] + [kb.dram(f"h{i}", [D, T], F32) for i in range(1, 3 * L + 1)]
    NP = ((L * MT * 3 + 63) // 64) * 64
    bmodp = kb.dram("bmodp", [128, NP], F32)
    gmod = kb.dram("gmod", [8 * 128, NP], F32)

    pst = kb.stage()
    cst = kb.sb(pst, "cst", [128, 4 * 128], F32)
    selt = kb.sb(pst, "selt", [128, 20], F32)
    modL = kb.sb(pst, "modL", [128, L, 9 * KD], F32)
    modC = kb.sb(pst, "modC", [128, L, 9 * KD], F32)
    AL = kb.sb(pst, "AL", [128, L, 3, KD], F32)
    AC = kb.sb(pst, "AC", [128, L, 3, KD], F32)
    GL = kb.sb(pst, "GL", [128, L, 3, KD], F32)
    GC = kb.sb(pst, "GC", [128, L, 3, KD], F32)
    ngt = kb.sb(pst, "ngt", [128, L, 3, KD], F32)
    SP.dma(cst, cst.t[:], consts, consts.t[:, :], cst)
    SP.dma(selt, selt.t[:], sel, sel.t[:, :], selt)
    SP.dma(ngt, ngt.t[:].rearrange("p a b c -> p (a b c)"), ng, ng.t[:, :], ngt)
    ones = cst.t[:, 0:128]

    st = kb.stage()
    cs = kb.sb(st, "cs", [128, KD * 3], F32)
    csil = kb.sb(st, "csil", [128, KD * 3], F32)
    bm = kb.sb(st, "bm", [128, L * MT], F32)
    modp = kb.sb(st, "modp", [128, NP], F32)
    modS = kb.sb(st, "modS", [128, 8, NP], F32)
    DVE.op(lambda e: e.memset(modp.t[:], 0.0), [], [modp])
    wts = [kb.sb(st, f"wmt{i}", [128, KD * 128], F32) for i in range(2)]
    SP.dma(cs, cs.t[:], cv, cv.t[:, :], cs)
    SP.dma(bm, bm.t[:], bmod, bmod.t[:, :], bm)
    ACT.op(lambda e: e.activation(out=csil.t[:], in_=cs.t[:], func=AF.Silu), [cs], [csil])
    for t in (range(L * MT) if 'mod' in dbg else []):
        wt = wts[t % 2]
        SP.dma(wt, wt.t[:], wmod, wmod.t[t * 128:(t + 1) * 128, :], wt)
        ps = kb.ps[t % 2]
        for kc in range(KD):
            PE.op(lambda e, kc=kc: e.matmul(ps.t[:, 0:3], lhsT=wt.t[:, kc * 128:(kc + 1) * 128],
                                            rhs=csil.t[:, kc * 3:(kc + 1) * 3], start=(kc == 0), stop=(kc == KD - 1)),
                  [wt, csil], [ps], signal=(kc == KD - 1))
        DVE.op(lambda e: e.tensor_scalar(out=modp.t[:, t * 3:(t + 1) * 3], in0=ps.t[:, 0:3], scalar1=bm.t[:, t:t + 1],
                                         scalar2=1.0, op0=ALU.add, op1=ALU.mult), [ps, bm], [modp])
    if 'mod' not in dbg:
        DVE.op(lambda e: e.memset(modp.t[:], 0.0), [], [modp])
    SP.dma(bmodp, bmodp.t[:, :], modp, modp.t[:], modp)
    if 'modag' in dbg:
        kb.allgather(bmodp, gmod)
    SP.dma(modS, modS.t[:], gmod, gmod.t[:, :].rearrange("(r p) c -> p r c", p=128), modS)
    for i in range(L):
        o = modL.t[:, i, :].rearrange("p (r j) -> p r j", r=8)
        DVE.op(lambda e: e.tensor_scalar(out=o, in0=modS.t[:, :, i * MT * 3 + 0:(i + 1) * MT * 3:3], scalar1=selt.t[:, 0:1], scalar2=1.0,
                                         op0=ALU.mult, op1=ALU.mult), [modS, selt], [modL])
        DVE.op(lambda e: e.scalar_tensor_tensor(out=o, in0=modS.t[:, :, i * MT * 3 + 1:(i + 1) * MT * 3:3], scalar=selt.t[:, 1:2], in1=o,
                                                op0=ALU.mult, op1=ALU.add), [modS, selt, modL], [modL])
        oc = modC.t[:, i, :].rearrange("p (r j) -> p r j", r=8)
        DVE.op(lambda e: e.tensor_copy(out=oc, in_=modS.t[:, :, i * MT * 3 + 2:(i + 1) * MT * 3:3]), [modS], [modC])
        for n in range(3):
            for (M, A_, G_) in ((modL, AL, GL), (modC, AC, GC)):
                DVE.op(lambda e, M=M, A_=A_: e.scalar_tensor_tensor(
                    out=A_.t[:, i, n, :], in0=M.t[:, i, (3 * n + 1) * KD:(3 * n + 2) * KD], scalar=1.0,
                    in1=ngt.t[:, i, n, :], op0=ALU.add, op1=ALU.mult), [M, ngt], [A_])
                gs = 1.0 if n == 1 else 0.5
                DVE.op(lambda e, M=M, G_=G_, gs=gs: e.tensor_scalar(
                    out=G_.t[:, i, n, :], in0=M.t[:, i, (3 * n + 2) * KD:(3 * n + 3) * KD], scalar1=gs, scalar2=1.0,
                    op0=ALU.mult, op1=ALU.mult), [M], [G_])
    kb.end_stage(st)

    def shiftL(i, n, kc, isctx):
        M = modC if isctx else modL
        return M.t[:, i, 3 * n * KD + kc:3 * n * KD + kc + 1]

    st = kb.stage()
    cin = [kb.sb(st, f"cin{i}", [128, 4096], F32) for i in range(2)]
    cout = [kb.sb(st, f"cout{i}", [128, 4096], BF16) for i in range(3)]
    blk = 0
    for f in (FAM_ORDER if 'cast' in dbg else []):
        nt, X = specs[f]
        rows = (nt // 8) * 128 * X // 512
        r0 = 0
        first = True
        while r0 < rows:
            nb = min(1024, rows - r0)
            a = nb // 128
            ci, co = cin[blk % 2], cout[blk % 3]
            SP.dma(ci, ci.t[:, 0:a * 512], fin[f], fin[f].t[r0:r0 + nb, :].rearrange("(p a) c -> p (a c)", p=128), ci)
            ce = (DVE, ACT, POOL)[blk % 3]
            if ce is ACT:
                ce.op(lambda e: e.copy(out=co.t[:, 0:a * 512], in_=ci.t[:, 0:a * 512]), [ci], [co])
            else:
                ce.op(lambda e: e.tensor_copy(out=co.t[:, 0:a * 512], in_=ci.t[:, 0:a * 512]), [ci], [co])
            SP.dma(bounce[f], bounce[f].t[r0:r0 + nb, :].rearrange("(p a) c -> p (a c)", p=128), co, co.t[:, 0:a * 512], co,
                     merge=not first)
            first = False
            r0 += nb
            blk += 1
        if 'castag' in dbg or ('castag1' in dbg and f == 'w1_00'):
            kb.allgather(bounce[f], gath[f])
    kb.end_stage(st)

    def rms_stats(st, hsrc, c0, c1, hld, sq, psb, rstd):
        w = c1 - c0
        for kc in range(KD):
            hl = hld[kc % len(hld)]
            SP.dma(hl, hl.t[:, 0:w], hsrc, hsrc.t[kc * 128:(kc + 1) * 128, c0:c1], hl)
            s = sq[kc % len(sq)]
            ACT.op(lambda e: e.activation(out=s.t[:, 0:w], in_=hl.t[:, 0:w], func=AF.Square), [hl], [s])
            PE.op(lambda e: e.matmul(psb.t[:, 0:w], lhsT=ones, rhs=s.t[:, 0:w], start=(kc == 0), stop=(kc == KD - 1)),
                  [s, cst], [psb])
        DVE.op(lambda e: e.tensor_scalar(out=rstd.t[:, 0:w], in0=psb.t[:, 0:w], scalar1=1.0 / D, scalar2=EPS,
                                         op0=ALU.mult, op1=ALU.add), [psb], [rstd])
        ACT.op(lambda e: e.activation(out=rstd.t[:, 0:w], in_=rstd.t[:, 0:w], func=AF.Sqrt), [rstd], [rstd])
        DVE.op(lambda e: e.reciprocal(out=rstd.t[:, 0:w], in_=rstd.t[:, 0:w]), [rstd], [rstd])

    def modulate(hsrc, c0, c1, i, n, isctx, hld, tmp, rstd, U, ucol0):
        w = c1 - c0
        A_ = AC if isctx else AL
        for kc in range(KD):
            hl = hld[kc % len(hld)]
            SP.dma(hl, hl.t[:, 0:w], hsrc, hsrc.t[kc * 128:(kc + 1) * 128, c0:c1], hl)
            tm = tmp[kc % len(tmp)]
            DVE.op(lambda e: e.tensor_tensor(out=tm.t[:, 0:w], in0=hl.t[:, 0:w], in1=rstd.t[:, 0:w], op=ALU.mult),
                   [hl, rstd], [tm])
            ACT.op(lambda e: e.activation(out=U.t[:, kc, ucol0:ucol0 + w], in_=tm.t[:, 0:w], func=AF.Identity,
                                          scale=A_.t[:, i, n, kc:kc + 1], bias=shiftL(i, n, kc, isctx)),
                   [tm, A_, modL, modC], [U])

    def ffn(i, j, hin, hout, tiles):
        n = 0 if j == 0 else 2
        gw1, gw2 = gath[f"w1_{i}{j}"], gath[f"w2_{i}{j}"]
        st = kb.stage()
        TWm = max(c1 - c0 for (c0, c1, _) in tiles)
        U = kb.sb(st, "U", [128, KD, TWm], BF16)
        act = kb.sb(st, "act", [128, FC, TWm], BF16)
        w1s = [kb.sb(st, f"w1s{q}", [128, KD, 256], BF16) for q in range(2)]
        H2 = FC // 2
        w2s = [kb.sb(st, f"w2s{q}", [128, H2, 128], BF16) for q in range(3)]
        hld = [kb.sb(st, f"hld{q}", [128, 512], F32) for q in range(3)]
        sq = [kb.sb(st, f"sq{q}", [128, 512], F32) for q in range(2)]
        rstd = kb.sb(st, "rstd", [128, 512], F32)
        sg = [kb.sb(st, f"sg{q}", [128, 512], F32) for q in range(2)]
        ot = [kb.sb(st, f"ot{q}", [128, 512], F32) for q in range(2)]
        cnt = 0
        for (c0, c1, isctx) in tiles:
            w = c1 - c0
            rms_stats(st, hin, c0, c1, hld, sq, kb.ps[0], rstd)
            modulate(hin, c0, c1, i, n, isctx, hld, sq, rstd, U, 0)
            if dbgo is not None and i == 0 and j == 0 and c0 == 0:
                d1 = kb.sb(st, "d1", [128, 512], F32)
                d2 = kb.sb(st, "d2", [128, 512], F32)
                SP.dma(dbgo, dbgo.t[:, 0:9 * KD], modL, modL.t[:, 0, :], d1, merge=True)
                DVE.op(lambda e: e.tensor_copy(out=d1.t[:, 0:w], in_=rstd.t[:, 0:w]), [rstd], [d1])
                SP.dma(dbgo, dbgo.t[:, 512:512 + w], d1, d1.t[:, 0:w], d1, merge=True)
                DVE.op(lambda e: e.tensor_copy(out=d2.t[:, 0:w], in_=U.t[:, 0, 0:w]), [U], [d2])
                SP.dma(dbgo, dbgo.t[:, 1024:1024 + w], d2, d2.t[:, 0:w], d2, merge=True)
            for t in (range(FC) if 'f2' in dbg else []):
                wt = w1s[t % 2]
                SP.dma(wt, wt.t[:].rearrange("p a b -> p (a b)"), gw1, tile_ap(gw1, t, KD * 256), wt)
                pg, pu = kb.ps[1 + t % 2], kb.ps[3 + t % 2]
                for kc in range(KD):
                    PE.op(lambda e: e.matmul(pg.t[:, 0:w], lhsT=wt.t[:, kc, 0:128], rhs=U.t[:, kc, 0:w],
                                             start=(kc == 0), stop=(kc == KD - 1)), [wt, U], [pg], signal=(kc == KD - 1))
                for kc in range(KD):
                    PE.op(lambda e: e.matmul(pu.t[:, 0:w], lhsT=wt.t[:, kc, 128:256], rhs=U.t[:, kc, 0:w],
                                             start=(kc == 0), stop=(kc == KD - 1)), [wt, U], [pu], signal=(kc == KD - 1))
                s = sg[t % 2]
                ACT.op(lambda e: e.activation(out=s.t[:, 0:w], in_=pg.t[:, 0:w], func=AF.Silu), [pg], [s])
                DVE.op(lambda e: e.tensor_tensor(out=act.t[:, t, 0:w], in0=s.t[:, 0:w], in1=pu.t[:, 0:w], op=ALU.mult),
                       [s, pu], [act])
            if dbgo is not None and i == 0 and j == 0 and c0 == 0:
                d3 = kb.sb(st, "d3", [128, 512], F32)
                DVE.op(lambda e: e.tensor_copy(out=d3.t[:, 0:w], in_=act.t[:, 0, 0:w]), [act], [d3])
                SP.dma(dbgo, dbgo.t[:, 1536:1536 + w], d3, d3.t[:, 0:w], d3, merge=True)
            G_ = GC if isctx else GL
            for nn in (range(KD) if 'f3' in dbg else []):
                py = kb.ps[5 + nn % 2]
                for hh in range(2):
                    wt = w2s[(2 * nn + hh) % 3]
                    rpt = 128 * FC * 128 // 512
                    src = gw2.t[nn * rpt:(nn + 1) * rpt, :].rearrange("(p a) c -> p (a c)", p=128)[:, hh * H2 * 128:(hh + 1) * H2 * 128]
                    SP.dma(wt, wt.t[:].rearrange("p a b -> p (a b)"), gw2, src, wt)
                    for fc in range(H2):
                        g = hh * H2 + fc
                        PE.op(lambda e: e.matmul(py.t[:, 0:w], lhsT=wt.t[:, fc, :], rhs=act.t[:, g, 0:w],
                                                 start=(g == 0), stop=(g == FC - 1)), [wt, act], [py], signal=(g == FC - 1))
                if 'f3b' not in dbg:
                    continue
                hl = hld[nn % len(hld)]
                SP.dma(hl, hl.t[:, 0:w], hin, hin.t[nn * 128:(nn + 1) * 128, c0:c1], hl)
                o = ot[cnt % 2]
                cnt += 1
                DVE.op(lambda e: e.scalar_tensor_tensor(out=o.t[:, 0:w], in0=py.t[:, 0:w], scalar=G_.t[:, i, n, nn:nn + 1],
                                                        in1=hl.t[:, 0:w], op0=ALU.mult, op1=ALU.add), [py, G_, hl], [o])
                if 'f3c' in dbg:
                    SP.dma(hout, hout.t[nn * 128:(nn + 1) * 128, c0:c1], o, o.t[:, 0:w], o, merge=True)
        kb.end_stage(st)

    def finish(hsrc):
        st = kb.stage()
        tl = [kb.sb(st, f"fin{q}", [128, TL], F32) for q in range(2)]
        for kc in range(KD):
            t_ = tl[kc % 2]
            SP.dma(t_, t_.t[:], hsrc, hsrc.t[kc * 128:(kc + 1) * 128, 0:TL], t_)
            SP.dma(outT, outT.t[kc * 128:(kc + 1) * 128, :], t_, t_.t[:], t_, merge=True)
        deps = {}
        _merge(deps, outT.w)
        SP.wait(deps)
        kb.end_stage(st)

    RPC = cfg.RPC
    NS = RPC + 8
    NPAIR = NS // 2
    NCHK = 4 * T // 128
    onesblk = cst.t[:, 128:256]
    RhT = cst.t[:, 256:384]
    RrT = cst.t[:, 384:512]

    def ccdram(name, rows, cols):
        b = kb.dram(name, [rows, cols // 2], F32)
        b.tcc = b.t
        b.t = b.t.bitcast(BF16)
        return b

    gains = kb.dram("gains", [128, 32], F32, "ExternalInput")
    ropeh = kb.dram("ropeh", [128, 2 * TL], F32, "ExternalInput")
    roper = kb.dram("roper", [128, 2 * TL], F32, "ExternalInput")
    nab = kb.dram("nab", [128, 16 * 16 * 64], F32, "ExternalInput")
    naval = kb.dram("naval", [128, RPC * NPAIR], F32, "ExternalInput")
    gt = kb.sb(pst, "gt", [128, 32], F32)
    gsc = kb.sb(pst, "gsc", [128, 4], F32)
    onesb = kb.sb(pst, "onesb", [128, 128], BF16)
    SP.dma(gt, gt.t[:], gains, gains.t[:, :], gt)
    DVE.op(lambda e: e.tensor_copy(out=onesb.t[:], in_=cst.t[:, 0:128]), [cst], [onesb])
    for (dc, sc_, scl) in ((0, 0, 128 ** -0.5), (1, 2, 128 ** -0.5), (2, 4, 192 ** -0.5), (3, 6, 192 ** -0.5)):
        DVE.op(lambda e: e.tensor_scalar(out=gsc.t[:, dc:dc + 1], in0=gt.t[:, sc_:sc_ + 1], scalar1=scl, scalar2=1.0,
                                         op0=ALU.mult, op1=ALU.mult), [gt], [gsc])

    def headnorm(ps, w, onesap, inv_n, g_ap, gbuf, sqt, ps2, rt, out):
        ACT.op(lambda e: e.activation(out=sqt.t[:, 0:w], in_=ps.t[:, 0:w], func=AF.Square), [ps], [sqt])
        PE.op(lambda e: e.matmul(ps2.t[:, 0:w], lhsT=onesap, rhs=sqt.t[:, 0:w], start=True, stop=True), [sqt, cst], [ps2])
        DVE.op(lambda e: e.tensor_scalar(out=rt.t[:, 0:w], in0=ps2.t[:, 0:w], scalar1=inv_n, scalar2=EPS,
                                         op0=ALU.mult, op1=ALU.add), [ps2], [rt])
        ACT.op(lambda e: e.activation(out=rt.t[:, 0:w], in_=rt.t[:, 0:w], func=AF.Sqrt), [rt], [rt])
        DVE.op(lambda e: e.reciprocal(out=rt.t[:, 0:w], in_=rt.t[:, 0:w]), [rt], [rt])
        DVE.op(lambda e: e.scalar_tensor_tensor(out=out.t[:, 0:w], in0=ps.t[:, 0:w], scalar=g_ap, in1=rt.t[:, 0:w],
                                                op0=ALU.mult, op1=ALU.mult), [ps, rt, gbuf], [out])

    def rope(x, w, RT, tab, c0, psr, t1, ob):
        PE.op(lambda e: e.matmul(psr.t[:, 0:w], lhsT=RT, rhs=x.t[:, 0:w], start=True, stop=True), [x, cst], [psr])
        DVE.op(lambda e: e.tensor_tensor(out=t1.t[:, 0:w], in0=x.t[:, 0:w], in1=tab.t[:, 0, c0:c0 + w], op=ALU.mult), [x, tab], [t1])
        DVE.op(lambda e: e.tensor_tensor(out=x.t[:, 0:w], in0=psr.t[:, 0:w], in1=tab.t[:, 1, c0:c0 + w], op=ALU.mult), [psr, tab], [x])
        DVE.op(lambda e: e.tensor_tensor(out=ob.t[:, 0:w], in0=t1.t[:, 0:w], in1=x.t[:, 0:w], op=ALU.add), [t1, x], [ob])

    def select2(out_b, out_ap, c_b, a0, a1):
        DVE.op(lambda e: e.tensor_scalar(out=out_ap, in0=a0, scalar1=selt.t[:, 0:1], scalar2=1.0, op0=ALU.mult, op1=ALU.mult),
               [c_b, selt], [out_b])
        DVE.op(lambda e: e.scalar_tensor_tensor(out=out_ap, in0=a1, scalar=selt.t[:, 1:2], in1=out_ap, op0=ALU.mult, op1=ALU.add),
               [c_b, selt, out_b], [out_b])

    def select8(out_b, out_ap, c_b, fn_r, col0):
        for r in range(8):
            if r == 0:
                DVE.op(lambda e: e.tensor_scalar(out=out_ap, in0=fn_r(r), scalar1=selt.t[:, col0:col0 + 1], scalar2=1.0,
                                                 op0=ALU.mult, op1=ALU.mult), [c_b, selt], [out_b])
            else:
                DVE.op(lambda e: e.scalar_tensor_tensor(out=out_ap, in0=fn_r(r), scalar=selt.t[:, col0 + r:col0 + r + 1], in1=out_ap,
                                                        op0=ALU.mult, op1=ALU.add), [c_b, selt, out_b], [out_b])

    def softmax_pv(chunks, w, qb, q_ap_fn, Pb, po, pd, cnt0, extra=None):
        n = len(chunks)

        def score(ci):
            kbuf, k_ap, vbuf, v_ap, mk, kr = chunks[ci]
            ps = kb.ps[4 + (cnt0 + ci) % 2]
            if kr is None:
                PE.op(lambda e: e.matmul(ps.t[:, 0:w], lhsT=k_ap, rhs=q_ap_fn(0), start=True, stop=True), [kbuf, qb], [ps])
            else:
                krb_, kr_ap, qrb, qr_ap = kr
                PE.op(lambda e: e.matmul(ps.t[:, 0:w], lhsT=k_ap, rhs=q_ap_fn(0), start=True, stop=False), [kbuf, qb], [ps], signal=False)
                PE.op(lambda e: e.matmul(ps.t[:, 0:w], lhsT=kr_ap, rhs=qr_ap, start=False, stop=True), [krb_, qrb], [ps])
            return ps
        pss = {0: score(0)}
        for ci in range(n):
            if ci + 1 < n:
                pss[ci + 1] = score(ci + 1)
            kbuf, k_ap, vbuf, v_ap, mk, kr = chunks[ci]
            ps = pss.pop(ci)
            pb = Pb[(cnt0 + ci) % 2]
            ACT.op(lambda e: e.activation(out=pb.t[:, 0:w], in_=ps.t[:, 0:w], func=AF.Exp), [ps], [pb])
            if mk is not None:
                DVE.op(lambda e: e.tensor_scalar(out=pb.t[:, 0:w], in0=pb.t[:, 0:w], scalar1=mk, scalar2=1.0, op0=ALU.mult, op1=ALU.mult),
                       [pb, selt], [pb])
            PE.op(lambda e: e.matmul(po.t[:, 0:w], lhsT=v_ap, rhs=pb.t[:, 0:w], start=(ci == 0), stop=(ci == n - 1)), [vbuf, pb], [po],
                  signal=False)
            PE.op(lambda e: e.matmul(pd.t[:, 0:w], lhsT=onesb.t[:], rhs=pb.t[:, 0:w], start=(ci == 0), stop=(ci == n - 1)), [onesb, pb], [pd])

    def normalize(po, pd, w, rec, out_b, out_ap):
        DVE.op(lambda e: e.reciprocal(out=rec.t[:, 0:w], in_=pd.t[:, 0:w]), [pd], [rec])
        DVE.op(lambda e: e.tensor_tensor(out=out_ap, in0=po.t[:, 0:w], in1=rec.t[:, 0:w], op=ALU.mult), [po, rec], [out_b])

    def out_proj(i, last, fam, aT_d, hin, hout):
        st = kb.stage()
        A = kb.sb(st, "A", [128, 32, T], BF16)
        ncol = TL if last else T
        SP.dma(A, A.t[:, :, 0:ncol], aT_d, aT_d.t[:, 0:ncol].rearrange("(c p) t -> p c t", p=128), A)
        ws = [kb.sb(st, f"wo{q}", [128, 32, 128], BF16) for q in range(2)]
        hld = [kb.sb(st, f"hld{q}", [128, 512], F32) for q in range(3)]
        ot = [kb.sb(st, f"ot{q}", [128, 512], F32) for q in range(2)]
        cnt = 0
        for nn in range(KD):
            wt = ws[nn % 2]
            SP.dma(wt, wt.t[:].rearrange("p a b -> p (a b)"), gath[fam], tile_ap(gath[fam], nn, 32 * 128), wt)
            for (c0, c1, isctx) in (cfg.ltiles if last else cfg.tiles):
                w = c1 - c0
                ps = kb.ps[cnt % 2]
                for c in range(32):
                    PE.op(lambda e: e.matmul(ps.t[:, 0:w], lhsT=wt.t[:, c, :], rhs=A.t[:, c, c0:c1], start=(c == 0), stop=(c == 31)),
                          [wt, A], [ps], signal=(c == 31))
                hl = hld[cnt % 3]
                SP.dma(hl, hl.t[:, 0:w], hin, hin.t[nn * 128:(nn + 1) * 128, c0:c1], hl)
                o = ot[cnt % 2]
                G_ = GC if isctx else GL
                DVE.op(lambda e: e.scalar_tensor_tensor(out=o.t[:, 0:w], in0=ps.t[:, 0:w], scalar=G_.t[:, i, 1, nn:nn + 1],
                                                        in1=hl.t[:, 0:w], op0=ALU.mult, op1=ALU.add), [ps, G_, hl], [o])
                SP.dma(hout, hout.t[nn * 128:(nn + 1) * 128, c0:c1], o, o.t[:, 0:w], o, merge=True)
                cnt += 1
        kb.end_stage(st)

    def modulate_all(st, hin, i, U):
        hld = [kb.sb(st, f"hld{q}", [128, 512], F32) for q in range(3)]
        sq = [kb.sb(st, f"sq{q}", [128, 512], F32) for q in range(2)]
        rstd = kb.sb(st, "rstd", [128, 512], F32)
        for (c0, c1, isctx) in cfg.tiles:
            rms_stats(st, hin, c0, c1, hld, sq, kb.ps[0], rstd)
            modulate(hin, c0, c1, i, 1, isctx, hld, sq, rstd, U, c0)

    def proj_tokmajor(st, U, KC, fam, ngroups, dst, dcol0, ucols=T):
        wv = [kb.sb(st, f"wv{q}", [128, KC, 256], BF16) for q in range(2)]
        vo = [kb.sb(st, f"vo{q}", [128, 256], BF16) for q in range(2)]
        cnt = 0
        for g in range(ngroups):
            for hf in range(2):
                wt = wv[(2 * g + hf) % 2]
                src = tile_ap(gath[fam], g, KC * 512).rearrange("p (a b) -> p a b", b=512)[:, :, hf * 256:(hf + 1) * 256]
                SP.dma(wt, wt.t[:], gath[fam], src, wt)
                for tok0 in range(0, ucols, 128):
                    nt = min(128, ucols - tok0)
                    ps = kb.ps[cnt % 2]
                    for kc in range(KC):
                        PE.op(lambda e: e.matmul(ps.t[0:nt, 0:256], lhsT=U.t[:, kc, tok0:tok0 + nt], rhs=wt.t[:, kc, :],
                                                 start=(kc == 0), stop=(kc == KC - 1)), [U, wt], [ps], signal=(kc == KC - 1))
                    o = vo[cnt % 2]
                    ACT.op(lambda e: e.copy(out=o.t[0:nt, :], in_=ps.t[0:nt, 0:256]), [ps], [o])
                    cc0 = dcol0 + g * 512 + hf * 256
                    SP.dma(dst, dst.t[tok0:tok0 + nt, cc0:cc0 + 256], o, o.t[0:nt, :], o, merge=True)
                    cnt += 1

    def mixer_even(i, last, hin, hout):
        qT_d = kb.dram(f"qT{i}", [32 * 128, T], BF16)
        aT_d = kb.dram(f"aT{i}", [32 * 128, T], BF16)
        kbb = ccdram(f"kbb{i}", 20 * 128, T)
        gk = ccdram(f"gk{i}", 8 * 20 * 128, T)
        vbb = ccdram(f"vbb{i}", T, 2560)
        gv = ccdram(f"gv{i}", 8 * T, 2560)
        st = kb.stage()
        U = kb.sb(st, "U", [128, KD, T], BF16)
        modulate_all(st, hin, i, U)
        ws = [kb.sb(st, f"ws{q}", [128, KD, 128], BF16) for q in range(3)]
        sqt = kb.sb(st, "sqt", [128, 512], F32)
        rt = kb.sb(st, "rt", [128, 512], F32)
        xn = [kb.sb(st, f"xn{q}", [128, 512], F32) for q in range(2)]
        t1 = kb.sb(st, "t1", [128, 512], F32)
        ob = [kb.sb(st, f"ob{q}", [128, 512], BF16) for q in range(2)]
        tabh = kb.sb(st, "tabh", [128, 2, TL], F32)
        SP.dma(tabh, tabh.t[:].rearrange("p a b -> p (a b)"), ropeh, ropeh.t[:, :], tabh)
        cnt = 0
        for t in range(52):
            wt = ws[t % 3]
            SP.dma(wt, wt.t[:].rearrange("p a b -> p (a b)"), gath["win_a"], tile_ap(gath["win_a"], t, KD * 128), wt)
            if t < 16:
                g_ap, gbuf, rp, dst, di = gsc.t[:, 0:1], gsc, False, qT_d, t
            elif t < 32:
                g_ap, gbuf, rp, dst, di = gsc.t[:, 1:2], gsc, True, qT_d, t
            elif t < 48:
                g_ap, gbuf, rp, dst, di = gt.t[:, 1:2], gt, False, kbb, t - 32
            else:
                g_ap, gbuf, rp, dst, di = gt.t[:, 3:4], gt, True, kbb, t - 32
            for (c0, c1, isctx) in cfg.tiles:
                w = c1 - c0
                ps = kb.ps[cnt % 2]
                for kc in range(KD):
                    PE.op(lambda e: e.matmul(ps.t[:, 0:w], lhsT=wt.t[:, kc, :], rhs=U.t[:, kc, c0:c1], start=(kc == 0), stop=(kc == KD - 1)),
                          [wt, U], [ps], signal=(kc == KD - 1))
                x = xn[cnt % 2]
                headnorm(ps, w, ones, 1.0 / 128, g_ap, gbuf, sqt, kb.ps[2], rt, x)
                o = ob[cnt % 2]
                if rp and not isctx:
                    rope(x, w, RhT, tabh, c0, kb.ps[3], t1, o)
                else:
                    ACT.op(lambda e: e.copy(out=o.t[:, 0:w], in_=x.t[:, 0:w]), [x], [o])
                SP.dma(dst, dst.t[di * 128:(di + 1) * 128, c0:c1], o, o.t[:, 0:w], o, merge=True)
                cnt += 1
        proj_tokmajor(st, U, KD, "win_v", 5, vbb, 0)
        kb.allgather(kbb, gk)
        kb.allgather(vbb, gv)
        kb.end_stage(st)
        st = kb.stage()
        gsrc_k = gk.t[:, :].rearrange("(r q) c -> q r c", r=8)
        nabt = [kb.sb(st, f"nabt{q}", [128, 16, 64], F32) for q in range(2)]
        E2 = [kb.sb(st, f"E2{q}", [128, 16, 64], F32) for q in range(2)]
        valt = kb.sb(st, "valt", [128, RPC, NPAIR], F32)
        SP.dma(valt, valt.t[:].rearrange("p a b -> p (a b)"), naval, naval.t[:, :], valt)
        KBt = [kb.sb(st, f"KB{q}", [128, NS * 64], BF16) for q in range(2)]
        candk = [kb.sb(st, f"candk{q}", [128, 8, 512], BF16) for q in range(2)]
        candc = [kb.sb(st, f"candc{q}", [128, 8, TC], BF16) for q in range(2)]
        Kc = [kb.sb(st, f"Kc{q}", [128, 4, TC], BF16) for q in range(2)]
        VBt = [kb.sb(st, f"VB{q}", [128, NPAIR, 128], BF16) for q in range(2)]
        candv = [kb.sb(st, f"candv{q}", [128, 8, 4, 128], BF16) for q in range(2)]
        candvc = [kb.sb(st, f"candvc{q}", [128, 4, 128], BF16) for q in range(2)]
        Vc = [kb.sb(st, f"Vc{q}", [128, 2, 128], BF16) for q in range(2)]
        qh = [kb.sb(st, f"qh{q}", [128, T], BF16) for q in range(2)]
        Pf = [kb.sb(st, f"Pf{q}", [128, 512], F32) for q in range(2)]
        Pb = [kb.sb(st, f"Pb{q}", [128, 512], BF16) for q in range(2)]
        rec = kb.sb(st, "rec", [128, 512], F32)
        ah = [kb.sb(st, f"ah{q}", [128, T], BF16) for q in range(2)]
        po, pd = kb.ps[6], kb.ps[7]
        cnt = 0
        for h in range(16):
            kbuf, ck, cc, kc_, vbuf, cv_, cvc, vc_, q_, a_ = (KBt[h % 2], candk[h % 2], candc[h % 2], Kc[h % 2], VBt[h % 2],
                                                              candv[h % 2], candvc[h % 2], Vc[h % 2], qh[h % 2], ah[h % 2])
            hk = slice(h * 128, (h + 1) * 128)
            SP.dma(kbuf, kbuf.t[:, 256:256 + TL], kbb, kbb.t[hk, 0:TL], kbuf)
            SP.dma(ck, ck.t[:, :, 0:256], gk, gsrc_k[hk, :, TL - 256:TL], ck)
            SP.dma(ck, ck.t[:, :, 256:512], gk, gsrc_k[hk, :, 0:256], ck, merge=True)
            select8(kbuf, kbuf.t[:, 0:256], ck, lambda r: ck.t[:, r, 0:256], 2)
            select8(kbuf, kbuf.t[:, 256 + TL:512 + TL], ck, lambda r: ck.t[:, r, 256:512], 10)
            SP.dma(cc, cc.t[:], gk, gsrc_k[hk, :, TL:T], cc)
            select2(kc_, kc_.t[:], cc, cc.t[:, 0:4, :], cc.t[:, 4:8, :])
            gsrc_v = gv.t[:, hk].rearrange("(r t) d -> t r d", r=8)
            SP.dma(vbuf, vbuf.t[:, 2:2 + RPC // 2, :], vbb, vbb.t[0:TL, hk].rearrange("(c p) d -> p c d", p=128), vbuf)
            for c in range(2):
                SP.dma(cv_, cv_.t[:, :, c, :], gv, gsrc_v[TL - 256 + c * 128:TL - 128 + c * 128], cv_, merge=(c > 0))
                SP.dma(cv_, cv_.t[:, :, 2 + c, :], gv, gsrc_v[c * 128:(c + 1) * 128], cv_, merge=True)
            select8(vbuf, vbuf.t[:, 0:2, :], cv_, lambda r: cv_.t[:, r, 0:2, :], 2)
            select8(vbuf, vbuf.t[:, NPAIR - 2:NPAIR, :], cv_, lambda r: cv_.t[:, r, 2:4, :], 10)
            gsrc_vc = gsrc_v[TL:T].rearrange("t (j two) d -> t two j d", two=2)
            SP.dma(cvc, cvc.t[0:TC, :, :], gv, gsrc_vc[:, 0, :, :], cvc)
            SP.dma(cvc, cvc.t[TC:2 * TC, :, :], gv, gsrc_vc[:, 1, :, :], cvc, merge=True)
            select2(vc_, vc_.t[:], cvc, cvc.t[:, 0:2, :], cvc.t[:, 2:4, :])
            nb_, e2 = nabt[h % 2], E2[h % 2]
            SP.dma(nb_, nb_.t[:].rearrange("p a b -> p (a b)"), nab, nab.t[:, h * 1024:(h + 1) * 1024], nb_)
            ACT.op(lambda e: e.activation(out=e2.t[:], in_=nb_.t[:], func=AF.Exp), [nb_], [e2])
            SP.dma(q_, q_.t[:], qT_d, qT_d.t[hk, :], q_)
            kcf = kc_.t[:].rearrange("p a b -> p (a b)")
            for lr in range(RPC):
                lo, hi = min(lr, RPC - 4), max(lr, 4) + 7
                m0, m1 = lo // 2, hi // 2
                npair = m1 - m0 + 1
                nch = npair + 2
                ps = kb.ps[4 + cnt % 2]
                qa = q_.t[:, lr * 64:(lr + 1) * 64]
                for ci in range(nch):
                    k_ap = kbuf.t[:, (m0 + ci) * 128:(m0 + ci + 1) * 128] if ci < npair else kcf[:, (ci - npair) * 128:(ci - npair + 1) * 128]
                    PE.op(lambda e: e.matmul(ps.t[:, ci * 64:(ci + 1) * 64], lhsT=k_ap, rhs=qa, start=True, stop=True),
                          [kbuf, kc_, q_], [ps], signal=(ci == nch - 1))
                pf, pb = Pf[cnt % 2], Pb[cnt % 2]
                ACT.op(lambda e: e.activation(out=pf.t[:, 0:nch * 64], in_=ps.t[:, 0:nch * 64], func=AF.Exp), [ps], [pf])
                for ci in range(npair):
                    m = m0 + ci
                    d = 2 * m - lr + 3
                    DVE.op(lambda e: e.scalar_tensor_tensor(out=pb.t[:, ci * 64:(ci + 1) * 64], in0=pf.t[:, ci * 64:(ci + 1) * 64],
                                                            scalar=valt.t[:, lr, m:m + 1], in1=e2.t[:, d + 1, :],
                                                            op0=ALU.mult, op1=ALU.mult), [pf, valt, e2], [pb])
                DVE.op(lambda e: e.tensor_copy(out=pb.t[:, npair * 64:nch * 64], in_=pf.t[:, npair * 64:nch * 64]), [pf], [pb])
                for ci in range(nch):
                    v_ap = vbuf.t[:, m0 + ci, :] if ci < npair else vc_.t[:, ci - npair, :]
                    PE.op(lambda e: e.matmul(po.t[:, 0:64], lhsT=v_ap, rhs=pb.t[:, ci * 64:(ci + 1) * 64], start=(ci == 0), stop=(ci == nch - 1)),
                          [vbuf, vc_, pb], [po], signal=False)
                for ci in range(nch):
                    PE.op(lambda e: e.matmul(pd.t[:, 0:64], lhsT=onesb.t[:], rhs=pb.t[:, ci * 64:(ci + 1) * 64], start=(ci == 0), stop=(ci == nch - 1)),
                          [onesb, pb], [pd], signal=(ci == nch - 1))
                normalize(po, pd, 64, rec, a_, a_.t[:, lr * 64:(lr + 1) * 64])
                cnt += 1
            if not last:
                chunks = [(kc_, kcf[:, c * 128:(c + 1) * 128], vc_, vc_.t[:, c, :], None, None) for c in range(2)]
                softmax_pv(chunks, TC, q_, lambda _: q_.t[:, TL:T], Pb, po, pd, cnt)
                normalize(po, pd, TC, rec, a_, a_.t[:, TL:T])
            ncol = TL if last else T
            SP.dma(aT_d, aT_d.t[hk, 0:ncol], a_, a_.t[:, 0:ncol], a_, merge=True)
        Kg = kb.sb(st, "Kg", [128, 4, T], BF16)
        cg = kb.sb(st, "cg", [128, 8, T], BF16)
        Vg = kb.sb(st, "Vg", [128, NCHK, 128], BF16)
        cvg = kb.sb(st, "cvg", [128, 2, NCHK, 128], BF16)
        for kv in range(4):
            hk = slice((16 + kv) * 128, (17 + kv) * 128)
            SP.dma(cg, cg.t[:], gk, gsrc_k[hk, :, :], cg)
            select2(Kg, Kg.t[:], cg, cg.t[:, 0:4, :], cg.t[:, 4:8, :])
            vcol = slice(2048 + kv * 128, 2048 + (kv + 1) * 128)
            for b in range(2):
                SP.dma(cvg, cvg.t[:, b, :, :], gv, gv.t[b * 4 * T:(b + 1) * 4 * T, vcol].rearrange("(c p) d -> p c d", p=128), cvg, merge=(b > 0))
            select2(Vg, Vg.t[:], cvg, cvg.t[:, 0, :, :], cvg.t[:, 1, :, :])
            kgf = Kg.t[:].rearrange("p a b -> p (a b)")
            for g in range(4):
                hq = 16 + kv * 4 + g
                q_, a_ = qh[hq % 2], ah[hq % 2]
                SP.dma(q_, q_.t[:], qT_d, qT_d.t[hq * 128:(hq + 1) * 128, :], q_)
                for (c0, c1, isctx) in cfg.ltiles:
                    w = c1 - c0
                    chunks = [(Kg, kgf[:, c * 128:(c + 1) * 128], Vg, Vg.t[:, c, :], None, None) for c in range(NCHK)]
                    softmax_pv(chunks, w, q_, lambda _: q_.t[:, c0:c1], Pb, po, pd, 0)
                    normalize(po, pd, w, rec, a_, a_.t[:, c0:c1])
                if not last:
                    chunks = []
                    for rho in range(4):
                        fl = rho * T + TL
                        c, half = fl // 128, (fl % 128) // 64
                        chunks.append((Kg, kgf[:, c * 128:(c + 1) * 128], Vg, Vg.t[:, c, :], selt.t[:, 18 + half:19 + half], None))
                    softmax_pv(chunks, TC, q_, lambda _: q_.t[:, TL:T], Pb, po, pd, 0)
                    normalize(po, pd, TC, rec, a_, a_.t[:, TL:T])
                ncol = TL if last else T
                SP.dma(aT_d, aT_d.t[hq * 128:(hq + 1) * 128, 0:ncol], a_, a_.t[:, 0:ncol], a_, merge=True)
        kb.end_stage(st)
        out_proj(i, last, "wout", aT_d, hin, hout)

    def mixer_odd(i, last, hin, hout):
        qn_d = kb.dram(f"qn{i}", [32 * 128, T], BF16)
        qr_d = kb.dram(f"qr{i}", [16 * 128, T], BF16)
        oT_d = kb.dram(f"oT{i}", [32 * 128, T], BF16)
        knb = ccdram(f"knb{i}", 32 * 128, T)
        gkn = ccdram(f"gkn{i}", 8 * 32 * 128, T)
        vmb = ccdram(f"vmb{i}", T, 4096)
        gvm = ccdram(f"gvm{i}", 8 * T, 4096)
        krb = ccdram(f"krb{i}", 128, T)
        gkr = ccdram(f"gkr{i}", 8 * 128, T)
        cn_d = kb.dram(f"cn{i}", [12 * 128, T], BF16)
        st = kb.stage()
        U = kb.sb(st, "U", [128, KD, T], BF16)
        modulate_all(st, hin, i, U)
        ws = [kb.sb(st, f"ws{q}", [128, KD, 128], BF16) for q in range(2)]
        CQ = kb.sb(st, "CQ", [128, 12, 512], F32)
        CNs = [kb.sb(st, f"CNs{q}", [128, 12, 512], BF16) for q in range(1)]
        sqt = kb.sb(st, "sqt", [128, 512], F32)
        rt = kb.sb(st, "rt", [128, 512], F32)
        xn = [kb.sb(st, f"xn{q}", [128, 512], F32) for q in range(2)]
        t1 = kb.sb(st, "t1", [128, 512], F32)
        ob = [kb.sb(st, f"ob{q}", [128, 512], BF16) for q in range(2)]
        tabr = kb.sb(st, "tabr", [128, 2, TL], F32)
        SP.dma(tabr, tabr.t[:].rearrange("p a b -> p (a b)"), roper, roper.t[:, :], tabr)
        cnt = 0
        wcnt = 0
        for (c0, c1, isctx) in cfg.tiles:
            w = c1 - c0
            for t in range(13):
                wt = ws[wcnt % 2]
                wcnt += 1
                SP.dma(wt, wt.t[:].rearrange("p a b -> p (a b)"), gath["wdown"], tile_ap(gath["wdown"], t, KD * 128), wt)
                ps = kb.ps[cnt % 2]
                for kc in range(KD):
                    PE.op(lambda e: e.matmul(ps.t[:, 0:w], lhsT=wt.t[:, kc, :], rhs=U.t[:, kc, c0:c1], start=(kc == 0), stop=(kc == KD - 1)),
                          [wt, U], [ps], signal=(kc == KD - 1))
                if t < 12:
                    ACT.op(lambda e: e.copy(out=CQ.t[:, t, 0:w], in_=ps.t[:, 0:w]), [ps], [CQ])
                else:
                    x = xn[cnt % 2]
                    headnorm(ps, w, onesblk, 1.0 / 64, gt.t[:, 7:8], gt, sqt, kb.ps[2], rt, x)
                    o = ob[cnt % 2]
                    if not isctx:
                        rope(x, w, RrT, tabr, c0, kb.ps[3], t1, o)
                    else:
                        ACT.op(lambda e: e.copy(out=o.t[:, 0:w], in_=x.t[:, 0:w]), [x], [o])
                    SP.dma(krb, krb.t[:, c0:c1], o, o.t[:, 0:w], o, merge=True)
                cnt += 1
            CN_ = CNs[0]
            for (k0, nk, gcol) in ((0, 8, 8), (8, 4, 16)):
                ps2 = kb.ps[2]
                for c in range(nk):
                    ACT.op(lambda e: e.activation(out=sqt.t[:, 0:w], in_=CQ.t[:, k0 + c, 0:w], func=AF.Square), [CQ], [sqt])
                    PE.op(lambda e: e.matmul(ps2.t[:, 0:w], lhsT=ones, rhs=sqt.t[:, 0:w], start=(c == 0), stop=(c == nk - 1)), [sqt, cst], [ps2])
                DVE.op(lambda e: e.tensor_scalar(out=rt.t[:, 0:w], in0=ps2.t[:, 0:w], scalar1=1.0 / (nk * 128), scalar2=EPS,
                                                 op0=ALU.mult, op1=ALU.add), [ps2], [rt])
                ACT.op(lambda e: e.activation(out=rt.t[:, 0:w], in_=rt.t[:, 0:w], func=AF.Sqrt), [rt], [rt])
                DVE.op(lambda e: e.reciprocal(out=rt.t[:, 0:w], in_=rt.t[:, 0:w]), [rt], [rt])
                for c in range(nk):
                    DVE.op(lambda e: e.scalar_tensor_tensor(out=CN_.t[:, k0 + c, 0:w], in0=CQ.t[:, k0 + c, 0:w],
                                                            scalar=gt.t[:, gcol + c:gcol + c + 1], in1=rt.t[:, 0:w],
                                                            op0=ALU.mult, op1=ALU.mult), [CQ, gt, rt], [CN_])
            SP.dma(cn_d, cn_d.t[:, c0:c1].rearrange("(c p) t -> p c t", p=128), CN_, CN_.t[:, :, 0:w], CN_, merge=True)
        kb.end_stage(st)
        st = kb.stage()
        CN = kb.sb(st, "CN", [128, 12, T], BF16)
        SP.dma(CN, CN.t[:], cn_d, cn_d.t[:, :].rearrange("(c p) t -> p c t", p=128), CN)
        sqt = kb.sb(st, "sqt", [128, 512], F32)
        rt = kb.sb(st, "rt", [128, 512], F32)
        xn = [kb.sb(st, f"xn{q}", [128, 512], F32) for q in range(2)]
        t1 = kb.sb(st, "t1", [128, 512], F32)
        ob = [kb.sb(st, f"ob{q}", [128, 512], BF16) for q in range(2)]
        tabr = kb.sb(st, "tabr", [128, 2, TL], F32)
        SP.dma(tabr, tabr.t[:].rearrange("p a b -> p (a b)"), roper, roper.t[:, :], tabr)
        cnt = 0
        wq = [kb.sb(st, f"wq{q}", [128, 8, 128], BF16) for q in range(3)]
        for t in range(48):
            wt = wq[t % 3]
            SP.dma(wt, wt.t[:].rearrange("p a b -> p (a b)"), gath["wuq"], tile_ap(gath["wuq"], t, 8 * 128), wt)
            for (c0, c1, isctx) in cfg.ltiles:
                w = c1 - c0
                ps = kb.ps[cnt % 2]
                for kc in range(8):
                    PE.op(lambda e: e.matmul(ps.t[:, 0:w], lhsT=wt.t[:, kc, :], rhs=CN.t[:, kc, c0:c1], start=(kc == 0), stop=(kc == 7)),
                          [wt, CN], [ps], signal=(kc == 7))
                x = xn[cnt % 2]
                o = ob[cnt % 2]
                if t < 32:
                    headnorm(ps, w, ones, 1.0 / 128, gsc.t[:, 2:3], gsc, sqt, kb.ps[2], rt, x)
                    ACT.op(lambda e: e.copy(out=o.t[:, 0:w], in_=x.t[:, 0:w]), [x], [o])
                    SP.dma(qn_d, qn_d.t[t * 128:(t + 1) * 128, c0:c1], o, o.t[:, 0:w], o, merge=True)
                else:
                    headnorm(ps, w, onesblk, 1.0 / 64, gsc.t[:, 3:4], gsc, sqt, kb.ps[2], rt, x)
                    rope(x, w, RrT, tabr, c0, kb.ps[3], t1, o)
                    SP.dma(qr_d, qr_d.t[(t - 32) * 128:(t - 31) * 128, c0:c1], o, o.t[:, 0:w], o, merge=True)
                cnt += 1
        wk = [kb.sb(st, f"wk{q}", [128, 4, 128], BF16) for q in range(3)]
        for t in range(32):
            wt = wk[t % 3]
            SP.dma(wt, wt.t[:].rearrange("p a b -> p (a b)"), gath["wukv_k"], tile_ap(gath["wukv_k"], t, 4 * 128), wt)
            for (c0, c1, isctx) in cfg.tiles:
                w = c1 - c0
                ps = kb.ps[cnt % 2]
                for kc in range(4):
                    PE.op(lambda e: e.matmul(ps.t[:, 0:w], lhsT=wt.t[:, kc, :], rhs=CN.t[:, 8 + kc, c0:c1], start=(kc == 0), stop=(kc == 3)),
                          [wt, CN], [ps], signal=(kc == 3))
                x = xn[cnt % 2]
                o = ob[cnt % 2]
                headnorm(ps, w, ones, 1.0 / 128, gt.t[:, 5:6], gt, sqt, kb.ps[2], rt, x)
                ACT.op(lambda e: e.copy(out=o.t[:, 0:w], in_=x.t[:, 0:w]), [x], [o])
                SP.dma(knb, knb.t[t * 128:(t + 1) * 128, c0:c1], o, o.t[:, 0:w], o, merge=True)
                cnt += 1
        CKV = kb.sb(st, "CKV", [128, 4, T], BF16)
        DVE.op(lambda e: e.tensor_copy(out=CKV.t[:], in_=CN.t[:, 8:12, :]), [CN], [CKV])
        proj_tokmajor(st, CKV, 4, "wukv_v", 8, vmb, 0)
        kb.allgather(knb, gkn)
        kb.allgather(vmb, gvm)
        kb.allgather(krb, gkr)
        kb.end_stage(st)
        st = kb.stage()
        Kh = [kb.sb(st, f"Kh{q}", [128, 4, T], BF16) for q in range(2)]
        ckh = [kb.sb(st, f"ckh{q}", [128, 8, T], BF16) for q in range(2)]
        Vh = [kb.sb(st, f"Vh{q}", [128, NCHK, 128], BF16) for q in range(2)]
        cvh = [kb.sb(st, f"cvh{q}", [128, 2, NCHK, 128], BF16) for q in range(2)]
        KR = kb.sb(st, "KR", [128, 4, T], BF16)
        ckr = kb.sb(st, "ckr", [128, 8, T], BF16)
        qnh = [kb.sb(st, f"qnh{q}", [128, TL], BF16) for q in range(2)]
        qrh = [kb.sb(st, f"qrh{q}", [128, TL], BF16) for q in range(2)]
        Pb = [kb.sb(st, f"Pb{q}", [128, 512], BF16) for q in range(2)]
        rec = kb.sb(st, "rec", [128, 512], F32)
        oh = [kb.sb(st, f"oh{q}", [128, TL], BF16) for q in range(2)]
        po, pd = kb.ps[6], kb.ps[7]
        gsrc_kr = gkr.t[:, :].rearrange("(r q) c -> q r c", r=8)
        SP.dma(ckr, ckr.t[0:64, :, :], gkr, gsrc_kr[0:64], ckr)
        SP.dma(ckr, ckr.t[64:128, :, :], gkr, gsrc_kr[0:64], ckr, merge=True)
        select2(KR, KR.t[:], ckr, ckr.t[:, 0:4, :], ckr.t[:, 4:8, :])
        krf = KR.t[:].rearrange("p a b -> p (a b)")
        gsrc_kn = gkn.t[:, :].rearrange("(r q) c -> q r c", r=8)
        for h in range(32):
            kh_, ck_, vh_, cv_, qn_, qr_, o_ = Kh[h % 2], ckh[h % 2], Vh[h % 2], cvh[h % 2], qnh[h % 2], qrh[h % 2], oh[h % 2]
            hk = slice(h * 128, (h + 1) * 128)
            SP.dma(ck_, ck_.t[:], gkn, gsrc_kn[hk, :, :], ck_)
            select2(kh_, kh_.t[:], ck_, ck_.t[:, 0:4, :], ck_.t[:, 4:8, :])
            for b in range(2):
                SP.dma(cv_, cv_.t[:, b, :, :], gvm, gvm.t[b * 4 * T:(b + 1) * 4 * T, hk].rearrange("(c p) d -> p c d", p=128), cv_, merge=(b > 0))
            select2(vh_, vh_.t[:], cv_, cv_.t[:, 0, :, :], cv_.t[:, 1, :, :])
            SP.dma(qn_, qn_.t[:], qn_d, qn_d.t[hk, 0:TL], qn_)
            SP.dma(qr_, qr_.t[:], qr_d, qr_d.t[(h // 2) * 128:(h // 2 + 1) * 128, 0:TL], qr_)
            khf = kh_.t[:].rearrange("p a b -> p (a b)")
            hs_ = slice((h % 2) * 64, (h % 2) * 64 + 64)
            for (c0, c1, isctx) in cfg.ltiles:
                w = c1 - c0
                chunks = [(kh_, khf[:, c * 128:(c + 1) * 128], vh_, vh_.t[:, c, :], None,
                           (KR, krf[hs_, c * 128:(c + 1) * 128], qr_, qr_.t[hs_, c0:c1])) for c in range(NCHK)]
                softmax_pv(chunks, w, qn_, lambda _: qn_.t[:, c0:c1], Pb, po, pd, 0)
                normalize(po, pd, w, rec, o_, o_.t[:, c0:c1])
            SP.dma(oT_d, oT_d.t[hk, 0:TL], o_, o_.t[:], o_, merge=True)
        kb.end_stage(st)
        out_proj(i, last, "wo", oT_d, hin, hout)

    mixers = [mixer_even, mixer_odd]

    hcur = 0
    for i in (range(L) if 'ffn' in dbg else []):
        last = i == L - 1
        ffn(i, 0, hs[hcur], hs[hcur + 1], cfg.tiles)
        hcur += 1
        if stop_after == (i, 0):
            break
        mixers[i % 2](i, last, hs[hcur], hs[hcur + 1])
        hcur += 1
        if stop_after == (i, 1):
            break
        ffn(i, 1, hs[hcur], hs[hcur + 1], cfg.ltiles if last else cfg.tiles)
        hcur += 1
        if stop_after == (i, 2):
            break
    finish(hs[hcur])
    return nc


def _mixer_todo(kb, cfg, i, last, hin, hout, gath, env):
    st = kb.stage()
    tl = [kb.sb(st, f"mx{q}", [128, cfg.T], F32) for q in range(2)]
    for kc in range(cfg.KD):
        t_ = tl[kc % 2]
        kb.SP.dma(t_, t_.t[:], hin, hin.t[kc * 128:(kc + 1) * 128, :], t_)
        kb.SP.dma(hout, hout.t[kc * 128:(kc + 1) * 128, :], t_, t_.t[:], t_, merge=True)
    kb.end_stage(st)


mixer_even = _mixer_todo
mixer_odd = _mixer_todo


MIXERS = [mixer_even, mixer_odd]


def _tiles_fm(w, ncols_per_tile=128):
    K_, N = w.shape
    nt = N // ncols_per_tile
    a = w.reshape(K_ // 128, 128, nt, ncols_per_tile).transpose(2, 1, 0, 3)
    return np.ascontiguousarray(a).reshape(nt, 128, -1)


def _shard_rows(tiles, r):
    nt = tiles.shape[0]
    tpr = nt // 8
    return np.ascontiguousarray(tiles[r * tpr:(r + 1) * tpr]).reshape(-1, 512)


def _pad_tiles(tiles, n):
    if tiles.shape[0] == n:
        return tiles
    pad = np.zeros((n - tiles.shape[0],) + tiles.shape[1:], tiles.dtype)
    return np.concatenate([tiles, pad], 0)


def host_families(cfg, inp):
    KD, FC, F, D = cfg.KD, cfg.FC, cfg.F, cfg.D
    fams = {}
    for i in range(cfg.L):
        for j in range(2):
            w1 = inp["ffn_w1"][i, j]
            g = _tiles_fm(w1[:, :F]).reshape(FC, 128, KD, 128)
            u = _tiles_fm(w1[:, F:]).reshape(FC, 128, KD, 128)
            fams[f"w1_{i}{j}"] = np.concatenate([g, u], axis=3).reshape(FC, 128, KD * 256)
            fams[f"w2_{i}{j}"] = _tiles_fm(inp["ffn_w2"][i, j])
    win = inp["ev_w_in"][0]
    NAW = 2048
    cols_a = np.concatenate([np.arange(0, 4096), np.arange(4096, 4096 + NAW), np.arange(4096 + 2 * NAW, 4096 + 2 * NAW + 512)])
    fams["win_a"] = _pad_tiles(_tiles_fm(win[:, cols_a]), 56)
    cols_v = np.concatenate([np.arange(4096 + NAW, 4096 + 2 * NAW), np.arange(4096 + 2 * NAW + 512, 4096 + 2 * NAW + 1024)])
    fams["win_v"] = _pad_tiles(_tiles_fm(win[:, cols_v], 512), 8)
    fams["wout"] = _tiles_fm(inp["ev_w_out"][0])
    wd = inp["mla_w_down"][0]
    wdp = np.concatenate([wd, np.zeros((D, 64), np.float32)], 1)
    fams["wdown"] = _pad_tiles(_tiles_fm(wdp), 16)
    wuq = inp["mla_w_uq"][0].reshape(1024, 32, 192)
    nope = wuq[:, :, :128].reshape(1024, 4096)
    rope = wuq[:, :, 128:].reshape(1024, 2048)
    fams["wuq"] = np.concatenate([_tiles_fm(nope), _tiles_fm(rope)], 0)
    wukv = inp["mla_w_ukv"][0].reshape(512, 32, 256)
    fams["wukv_k"] = _tiles_fm(wukv[:, :, :128].reshape(512, 4096))
    fams["wukv_v"] = _tiles_fm(wukv[:, :, 128:].reshape(512, 4096), 512)
    fams["wo"] = _tiles_fm(inp["mla_w_o"][0])
    return fams


def host_consts():
    c = np.zeros((128, 4, 128), np.float32)
    c[:, 0, :] = 1.0
    c[:64, 1, :64] = 1.0
    c[64:, 1, 64:] = 1.0
    Rh = np.zeros((128, 128), np.float32)
    for a in range(64):
        Rh[a, a + 64] = -1.0
        Rh[a + 64, a] = 1.0
    c[:, 2, :] = Rh.T
    Rr = np.zeros((128, 128), np.float32)
    for blk in range(2):
        for a in range(32):
            Rr[blk * 64 + a, blk * 64 + a + 32] = -1.0
            Rr[blk * 64 + a + 32, blk * 64 + a] = 1.0
    c[:, 3, :] = Rr.T
    return c.reshape(128, 512)


def host_inputs(cfg, inp):
    KD, MT, L, TL, TC = cfg.KD, cfg.MT, cfg.L, cfg.TL, cfg.TC
    fams = host_families(cfg, inp)
    cvec = np.stack([inp["c"][0], inp["c"][1], inp["c_ctx"]], -1)
    cv = np.ascontiguousarray(cvec.reshape(KD, 128, 3).transpose(1, 0, 2)).reshape(128, KD * 3)
    ngh = np.ascontiguousarray(inp["norm_g"].reshape(L, 3, KD, 128).transpose(3, 0, 1, 2)).reshape(128, -1)
    consts = host_consts()
    maps = []
    for r in range(NCORES):
        b, q4 = divmod(r, 4)
        m = {}
        xl = inp["x"][b, q4 * TL:(q4 + 1) * TL, :]
        xc = inp["ctx"][b, q4 * TC:(q4 + 1) * TC, :]
        m["xT"] = np.ascontiguousarray(np.concatenate([xl, xc], 0).T)
        m["cv"] = cv
        s = np.zeros((128, 18), np.float32)
        s[:, b] = 1.0
        if q4 > 0:
            s[:, 2 + r - 1] = 1.0
        if q4 < 3:
            s[:, 10 + r + 1] = 1.0
        m["sel"] = s
        wm = inp["w_mod"][:, :, r * MT * 128:(r + 1) * MT * 128]
        wm = wm.reshape(L, KD, 128, MT, 128).transpose(0, 3, 2, 1, 4)
        m["wmod"] = np.ascontiguousarray(wm).reshape(L * MT * 128, KD * 128)
        bmr = inp["b_mod"][:, r * MT * 128:(r + 1) * MT * 128].reshape(L, MT, 128).transpose(2, 0, 1)
        m["bmod"] = np.ascontiguousarray(bmr).reshape(128, L * MT)
        m["ng"] = ngh
        m["consts"] = consts
        for f in FAM_ORDER:
            m["f_" + f] = _shard_rows(fams[f], r)
        maps.append(m)
    return host_mixer_inputs(cfg, inp, maps)


def assemble(cfg, results):
    out = np.zeros((2, cfg.S, cfg.D), np.float32)
    for r in range(NCORES):
        b, q4 = divmod(r, 4)
        out[b, q4 * cfg.TL:(q4 + 1) * cfg.TL, :] = results[r]["outT"].T
    return out


def kernel(**inputs):
    cfg = Cfg()
    inp = {k: np.asarray(v) for k, v in inputs.items()}
    nc = build(cfg)
    maps = host_inputs(cfg, inp)
    res = run_bass_kernel_spmd(nc, maps, core_ids=list(range(NCORES)))
    return assemble(cfg, res.results)


def _axial(S, rot_dim):
    t = np.arange(S)
    row = (t // 64).astype(np.float32)
    col = (t % 64).astype(np.float32)
    axis_dim = rot_dim // 2
    inv = (np.float32(10000.0) ** (-(np.arange(0, axis_dim, 2, dtype=np.float32) / np.float32(axis_dim)))).astype(np.float32)
    ang = np.concatenate([row[:, None] * inv, col[:, None] * inv], -1).astype(np.float32)
    return np.cos(ang).astype(np.float32), np.sin(ang).astype(np.float32)


def host_mixer_inputs(cfg, inp, maps):
    TL, RPC = cfg.TL, cfg.RPC
    NPAIR = (RPC + 8) // 2
    rows = cfg.S // 64
    g = np.zeros((128, 32), np.float32)
    g[:, 0] = inp["na_q_g"][0]
    g[:, 1] = inp["na_k_g"][0]
    g[:, 2] = inp["gq_q_g"][0]
    g[:, 3] = inp["gq_k_g"][0]
    g[:, 4] = inp["mla_qn_g"][0]
    g[:, 5] = inp["mla_kn_g"][0]
    g[:, 6] = np.concatenate([inp["mla_qr_g"][0]] * 2)
    g[:, 7] = np.concatenate([inp["mla_kr_g"][0]] * 2)
    g[:, 8:16] = inp["mla_q_a_g"][0].reshape(8, 128).T
    g[:, 16:20] = inp["mla_kv_a_g"][0].reshape(4, 128).T
    ch, sh = _axial(cfg.S, 128)
    cr, sr = _axial(cfg.S, 64)
    rpb = inp["na_rpb"][0]
    kc = np.arange(64)[:, None]
    q = np.arange(64)[None, :]
    cidx = np.clip(kc - q + 15, 0, 30)
    c0 = np.clip(np.arange(64) - 8, 0, 48)[None, :]
    cmask = (kc >= c0) & (kc < c0 + 16)
    nabt = np.zeros((128, 16, 16, 64), np.float32)
    for e in range(16):
        for half, d in ((0, e - 1), (1, e)):
            if 0 <= d <= 14:
                v = rpb[:, d, :][:, cidx]
                v = np.where(cmask[None], v, np.float32(-30000.0))
                nabt[half * 64:(half + 1) * 64, :, e, :] = v.transpose(1, 0, 2)
    nabt = nabt.reshape(128, -1)
    for r in range(NCORES):
        b, q4 = divmod(r, 4)
        m = maps[r]
        m["gains"] = g
        pos = np.arange(q4 * TL, (q4 + 1) * TL)
        m["ropeh"] = np.ascontiguousarray(np.stack([np.tile(ch[pos].T, (2, 1)), np.tile(sh[pos].T, (2, 1))], 1)).reshape(128, 2 * TL)
        m["roper"] = np.ascontiguousarray(np.stack([np.tile(cr[pos].T, (4, 1)), np.tile(sr[pos].T, (4, 1))], 1)).reshape(128, 2 * TL)
        m["nab"] = nabt
        val = np.zeros((128, RPC, NPAIR), np.float32)
        base = q4 * RPC
        for lr in range(RPC):
            lb = int(np.clip(base + lr - 4, 0, rows - 8)) - base + 4
            for mm in range(NPAIR):
                for half in range(2):
                    slot = 2 * mm + half
                    if lb <= slot < lb + 8:
                        val[half * 64:(half + 1) * 64, lr, mm] = 1.0
        m["naval"] = val.reshape(128, -1)
        s = np.zeros((128, 20), np.float32)
        s[:, :18] = m["sel"]
        s[:64, 18] = 1.0
        s[64:, 19] = 1.0
        m["sel"] = s
    return maps
```
